# Optimizing a Trainium2 kernel written in Bass

```python
import math
import jax, jax.numpy as jnp
from jax import lax
import numpy as np

D_MODEL = 1024
BATCH = 1
SEQ = 16384
DEPTH = 4
DEC_BATCH = 8
DEC_SEQ = 32
PAST_LEN = 4096

CHUNK = 64
N_HEADS = 8
HEAD_DIM = 64
D_ATTN = N_HEADS * 2 * HEAD_DIM
D_CONV = D_MODEL
CONV_WIDTH = 31
CONV_STATE = CONV_WIDTH - 1
D_FF = -(-8 * D_MODEL // (3 * 256)) * 256
N_BUCKETS = 32
MAX_DISTANCE = 128
Q_BLOCK = 128
RMS_EPS = 1e-6
LN_EPS = 1e-5
SUBLN_EPS = 1e-5
NEG_INF = -1e30

OFF_Q = 2 * D_CONV
OFF_K = OFF_Q + D_ATTN
OFF_V = OFF_K + D_ATTN
OFF_GATE = OFF_V + D_ATTN
D_IN = OFF_GATE + 2 * D_MODEL

kernel_name = "hybrid_conformer_conv_diff_attn_stream_step"


def rmsnorm(x, g, eps=RMS_EPS):
    xf = x.astype(jnp.float32)
    y = xf * lax.rsqrt(jnp.mean(xf * xf, axis=-1, keepdims=True) + eps)
    return (y * g.astype(jnp.float32)).astype(x.dtype)


def layernorm(x, g, b):
    xf = x.astype(jnp.float32)
    mu = jnp.mean(xf, axis=-1, keepdims=True)
    var = jnp.mean(jnp.square(xf - mu), axis=-1, keepdims=True)
    y = (xf - mu) * lax.rsqrt(var + LN_EPS)
    return (y * g.astype(jnp.float32) + b.astype(jnp.float32)).astype(x.dtype)


def t5_bucket(rel):
    half = N_BUCKETS // 2
    max_exact = half // 2
    ret = jnp.where(rel > 0, half, 0)
    n = jnp.abs(rel)
    nf = jnp.maximum(n, 1).astype(jnp.float32)
    large = max_exact + (jnp.log(nf / max_exact) / math.log(MAX_DISTANCE / max_exact)
                         * (half - max_exact)).astype(jnp.int32)
    large = jnp.minimum(large, half - 1)
    return ret + jnp.where(n < max_exact, n, large)


def diff_attn_block(q, k, v, q_pos, k_pos, lam, rel_bias):
    s = jnp.einsum('bqhmd,bkhmd->bhmqk', q, k,
                   preferred_element_type=jnp.float32) * (HEAD_DIM ** -0.5)
    bucket = t5_bucket(k_pos[None, :] - q_pos[:, None])
    bias = jnp.transpose(rel_bias.astype(jnp.float32)[bucket], (2, 0, 1))
    allowed = (k_pos[None, :] // CHUNK) <= (q_pos[:, None] // CHUNK)
    s = jnp.where(allowed, s + bias[None, :, None], NEG_INF)
    p = jax.nn.softmax(s, axis=-1)
    a = p[:, :, 0] - lam * p[:, :, 1]
    return jnp.einsum('bhqk,bkhe->bqhe', a.astype(v.dtype), v)


def diff_attention(q, k, v, q_pos, k_pos, lam, rel_bias):
    B, Tq = q.shape[0], q.shape[1]
    if Tq > Q_BLOCK and Tq % Q_BLOCK == 0:
        nb = Tq // Q_BLOCK
        qb = jnp.moveaxis(q.reshape(B, nb, Q_BLOCK, N_HEADS, 2, HEAD_DIM), 1, 0)
        pb = q_pos.reshape(nb, Q_BLOCK)
        ob = lax.map(lambda a: diff_attn_block(a[0], k, v, a[1], k_pos, lam, rel_bias), (qb, pb))
        return jnp.moveaxis(ob, 0, 1).reshape(B, Tq, N_HEADS, 2 * HEAD_DIM)
    return diff_attn_block(q, k, v, q_pos, k_pos, lam, rel_bias)


def layer(l, x, q_pos, k_past, v_past, conv_past, p):
    B, T = x.shape[0], x.shape[1]
    h = rmsnorm(x, p['norm_mix'][l])
    z = h @ p['w_in'][l]

    u = z[..., :OFF_Q]
    glu = u[..., :D_CONV] * jax.nn.sigmoid(u[..., D_CONV:])
    padded = jnp.concatenate([conv_past.astype(glu.dtype), glu], axis=1)
    dw = lax.conv_general_dilated(
        padded, p['conv_dw'][l][:, None, :].astype(padded.dtype), (1,), 'VALID',
        dimension_numbers=('NWC', 'WIO', 'NWC'), feature_group_count=D_CONV)
    dw = dw + p['conv_dw_b'][l]
    c = jax.nn.silu(layernorm(dw, p['conv_ln_g'][l], p['conv_ln_b'][l]))
    a_out = c @ p['w_conv_out'][l]
    conv_new = padded[:, -CONV_STATE:]

    q = z[..., OFF_Q:OFF_K].reshape(B, T, N_HEADS, 2, HEAD_DIM)
    k_new = z[..., OFF_K:OFF_V].reshape(B, T, N_HEADS, 2 * HEAD_DIM)
    v_new = z[..., OFF_V:OFF_GATE].reshape(B, T, N_HEADS, 2 * HEAD_DIM)
    if k_past is None:
        k_all, v_all, k_pos = k_new, v_new, q_pos
    else:
        k_all = jnp.concatenate([k_past.astype(k_new.dtype), k_new], axis=1)
        v_all = jnp.concatenate([v_past.astype(v_new.dtype), v_new], axis=1)
        k_pos = jnp.arange(k_all.shape[1], dtype=jnp.int32)
    Tk = k_all.shape[1]
    lambda_init = 0.8 - 0.6 * math.exp(-0.3 * l)
    lam = (jnp.exp(jnp.sum(p['lambda_q1'][l].astype(jnp.float32) * p['lambda_k1'][l].astype(jnp.float32)))
           - jnp.exp(jnp.sum(p['lambda_q2'][l].astype(jnp.float32) * p['lambda_k2'][l].astype(jnp.float32)))
           + lambda_init)
    o = diff_attention(q, k_all.reshape(B, Tk, N_HEADS, 2, HEAD_DIM), v_all,
                       q_pos, k_pos, lam, p['rel_bias'])
    o = rmsnorm(o, p['attn_subln_g'][l], SUBLN_EPS) * (1.0 - lambda_init)
    b_out = o.reshape(B, T, D_ATTN) @ p['w_attn_o'][l]

    gates = jax.nn.sigmoid(z[..., OFF_GATE:])
    merged = gates[..., :D_MODEL] * a_out + gates[..., D_MODEL:] * b_out
    x = x + merged @ p['w_out'][l]

    h2 = rmsnorm(x, p['norm_ffn'][l])
    gu = h2 @ p['w_ffn_in'][l]
    x = x + (jax.nn.silu(gu[..., :D_FF]) * gu[..., D_FF:]) @ p['w_ffn_out'][l]
    return x, k_new, v_new, conv_new


def setup_inputs(seed: int = 0) -> dict:
    key = jax.random.key(seed)
    ks = jax.random.split(key, 24)

    def nrm(k, shape, scale):
        return jax.random.normal(k, shape, jnp.float32) * scale

    return {
        "x_prompt": nrm(ks[0], (BATCH, SEQ, D_MODEL), 1.0),
        "x_sample": nrm(ks[1], (DEC_BATCH, DEC_SEQ, D_MODEL), 1.0),
        "cache_k": nrm(ks[2], (DEPTH, DEC_BATCH, PAST_LEN, N_HEADS, 2 * HEAD_DIM), 1.0),
        "cache_v": nrm(ks[3], (DEPTH, DEC_BATCH, PAST_LEN, N_HEADS, 2 * HEAD_DIM), 1.0),
        "state_conv": nrm(ks[4], (DEPTH, DEC_BATCH, CONV_STATE, D_CONV), 0.5),
        "norm_mix": 1.0 + nrm(ks[5], (DEPTH, D_MODEL), 0.02),
        "w_in": nrm(ks[6], (DEPTH, D_MODEL, D_IN), D_MODEL ** -0.5),
        "conv_dw": nrm(ks[7], (DEPTH, CONV_WIDTH, D_CONV), CONV_WIDTH ** -0.5),
        "conv_dw_b": nrm(ks[8], (DEPTH, D_CONV), 0.02),
        "conv_ln_g": 1.0 + nrm(ks[9], (DEPTH, D_CONV), 0.02),
        "conv_ln_b": nrm(ks[10], (DEPTH, D_CONV), 0.02),
        "w_conv_out": nrm(ks[11], (DEPTH, D_CONV, D_MODEL), D_CONV ** -0.5),
        "lambda_q1": nrm(ks[12], (DEPTH, HEAD_DIM), 0.1),
        "lambda_k1": nrm(ks[13], (DEPTH, HEAD_DIM), 0.1),
        "lambda_q2": nrm(ks[14], (DEPTH, HEAD_DIM), 0.1),
        "lambda_k2": nrm(ks[15], (DEPTH, HEAD_DIM), 0.1),
        "attn_subln_g": 1.0 + nrm(ks[16], (DEPTH, 2 * HEAD_DIM), 0.02),
        "w_attn_o": nrm(ks[17], (DEPTH, D_ATTN, D_MODEL), D_ATTN ** -0.5),
        "w_out": nrm(ks[18], (DEPTH, D_MODEL, D_MODEL), D_MODEL ** -0.5),
        "norm_ffn": 1.0 + nrm(ks[19], (DEPTH, D_MODEL), 0.02),
        "w_ffn_in": nrm(ks[20], (DEPTH, D_MODEL, 2 * D_FF), D_MODEL ** -0.5),
        "w_ffn_out": nrm(ks[21], (DEPTH, D_FF, D_MODEL), D_FF ** -0.5),
        "rel_bias": nrm(ks[22], (N_BUCKETS, N_HEADS), 0.5),
        "norm_final": 1.0 + nrm(ks[23], (D_MODEL,), 0.02),
    }


def reference(x_prompt, x_sample, cache_k, cache_v, state_conv,
              norm_mix, w_in, conv_dw, conv_dw_b, conv_ln_g, conv_ln_b, w_conv_out,
              lambda_q1, lambda_k1, lambda_q2, lambda_k2, attn_subln_g, w_attn_o,
              w_out, norm_ffn, w_ffn_in, w_ffn_out, rel_bias, norm_final):
    p = dict(norm_mix=norm_mix, w_in=w_in, conv_dw=conv_dw, conv_dw_b=conv_dw_b,
             conv_ln_g=conv_ln_g, conv_ln_b=conv_ln_b, w_conv_out=w_conv_out,
             lambda_q1=lambda_q1, lambda_k1=lambda_k1, lambda_q2=lambda_q2,
             lambda_k2=lambda_k2, attn_subln_g=attn_subln_g, w_attn_o=w_attn_o,
             w_out=w_out, norm_ffn=norm_ffn, w_ffn_in=w_ffn_in, w_ffn_out=w_ffn_out,
             rel_bias=rel_bias)

    T_p = x_prompt.shape[1]
    pos_p = jnp.arange(T_p, dtype=jnp.int32)
    conv0 = jnp.zeros((x_prompt.shape[0], CONV_STATE, D_CONV), x_prompt.dtype)
    xp = x_prompt
    kp, vp, cp = [], [], []
    for l in range(DEPTH):
        xp, k_n, v_n, c_n = layer(l, xp, pos_p, None, None, conv0, p)
        kp.append(k_n); vp.append(v_n); cp.append(c_n)
    y_prompt = rmsnorm(xp, norm_final)

    P = cache_k.shape[2]
    T_s = x_sample.shape[1]
    pos_s = P + jnp.arange(T_s, dtype=jnp.int32)
    xs = x_sample
    ks_, vs_, cs_ = [], [], []
    for l in range(DEPTH):
        xs, k_n, v_n, c_n = layer(l, xs, pos_s, cache_k[l], cache_v[l], state_conv[l], p)
        ks_.append(k_n); vs_.append(v_n); cs_.append(c_n)
    y_sample = rmsnorm(xs, norm_final)

    k_prompt = jnp.stack(kp)
    v_prompt = jnp.stack(vp)
    conv_prompt = jnp.stack(cp)
    k_sample = jnp.stack(ks_)
    v_sample = jnp.stack(vs_)
    conv_sample = jnp.stack(cs_)
    return (y_prompt, y_sample, k_prompt, v_prompt, conv_prompt, k_sample, v_sample, conv_sample)
```

```python
import contextlib
import math
import os
import numpy as np
import concourse.bass as bass
import concourse.mybir as mybir
from concourse.bass_utils import run_bass_kernel_spmd

F32 = mybir.dt.float32
BF16 = mybir.dt.bfloat16
AF = mybir.ActivationFunctionType
ALU = mybir.AluOpType
AX = mybir.AxisListType

NCORES = 8
D = 1024
DC = 8
H = 8
DFF = 2816
FC = 22
CW = 31
HALO = 30
NEG = -30000.0


class Buf:
    __slots__ = ("w", "r")

    def __init__(self):
        self.w = {}
        self.r = {}


class Q:
    def __init__(self, name, sem, is_pe=False):
        self.name = name
        self.sem = sem
        self.n = 0
        self.items = []
        self.waited = {}
        self.is_pe = is_pe


class FW:
    def __init__(self, nc, stack):
        self.nc = nc
        self.stack = stack
        self.sems = {}
        self.q = {}
        self.cnt = {}
        for name in ("pe", "act", "dve", "pool", "sp"):
            s = stack.enter_context(nc.semaphore("s_" + name))
            self.sems["s_" + name] = s
            self.cnt["s_" + name] = 0
            self.q[name] = Q(name, "s_" + name, is_pe=(name == "pe"))
        self.ph = None
        self._dsem_free = {False: [], True: []}
        self._nds = 0

    def dsem(self, sw=False):
        if self._dsem_free[sw]:
            return self._dsem_free[sw].pop()
        name = "d%d" % self._nds
        self._nds += 1
        s = self.stack.enter_context(self.nc.semaphore(name))
        self.sems[name] = s
        self.cnt[name] = 0
        return name

    def begin(self, ph):
        self.ph = ph
        self._phase_dsems = []

    def pdsem(self, sw=False):
        n = self.dsem(sw)
        self._phase_dsems.append((n, sw))
        return n

    def sb(self, name, shape, dt):
        self._uid = getattr(self, "_uid", 0) + 1
        return self.ph.enter_context(self.nc.sbuf_tensor("%s_u%d" % (name, self._uid), list(shape), dt))

    def op(self, qn, emit, reads=(), writes=(), dsem=None, ndma=1):
        q = self.q[qn]
        need = {}
        for b in reads:
            for k, v in b.w.items():
                if need.get(k, 0) < v:
                    need[k] = v
        for b in writes:
            for d in (b.w, b.r):
                for k, v in d.items():
                    if need.get(k, 0) < v:
                        need[k] = v
        waits = []
        for k, v in need.items():
            if q.is_pe and k == q.sem:
                continue
            if q.waited.get(k, 0) < v:
                q.waited[k] = v
                waits.append((k, v))
        if dsem is not None:
            self.cnt[dsem] += 16 * ndma
            tok = (dsem, self.cnt[dsem])
            q.items.append((waits, emit, dsem, 16))
        else:
            self.cnt[q.sem] += 1
            tok = (q.sem, self.cnt[q.sem])
            q.items.append((waits, emit, q.sem, 1))
        for b in writes:
            b.w = {tok[0]: tok[1]}
            b.r = {}
        for b in reads:
            if b.r.get(tok[0], 0) < tok[1]:
                b.r[tok[0]] = tok[1]
        return tok

    def barrier(self):
        for q in self.q.values():
            waits = []
            for k, v in self.cnt.items():
                if v > 0 and q.waited.get(k, 0) < v:
                    q.waited[k] = v
                    waits.append((k, v))
            q.items.append((waits, None, None, 0))

    def end(self):
        self.barrier()
        nc = self.nc
        sems = self.sems
        with nc.Block() as block:
            def run(q):
                items = q.items

                def body(eng):
                    for waits, emit, sname, amt in items:
                        for k, v in waits:
                            eng.wait_ge(sems[k], v)
                        if emit is None:
                            continue
                        r = emit(eng)
                        if isinstance(r, (list, tuple)):
                            for ins in r:
                                ins.then_inc(sems[sname], amt)
                        else:
                            r.then_inc(sems[sname], amt)
                return body
            block.tensor(run(self.q["pe"]))
            block.scalar(run(self.q["act"]))
            block.vector(run(self.q["dve"]))
            block.gpsimd(run(self.q["pool"]))
            block.sync(run(self.q["sp"]))
        for q in self.q.values():
            q.items = []
        for n, sw in self._phase_dsems:
            self._dsem_free[sw].append(n)
        self._phase_dsems = []


class Ring:
    def __init__(self, fw, name, shape, dt, n, dma=True):
        self.t = [fw.sb("%s%d" % (name, i), shape, dt) for i in range(n)]
        self.b = [Buf() for _ in range(n)]
        self.s = [fw.pdsem() if dma else None for _ in range(n)]
        self.n = n
        self.i = -1

    def next(self):
        self.i = (self.i + 1) % self.n
        return self.t[self.i], self.b[self.i], self.s[self.i]


def _t5_bucket(rel):
    half, max_exact = 16, 8
    ret = np.where(rel > 0, half, 0)
    n = np.abs(rel)
    nf = np.maximum(n, 1).astype(np.float32)
    large = max_exact + (np.log(nf / np.float32(max_exact)) / np.float32(math.log(128 / max_exact))
                         * np.float32(half - max_exact)).astype(np.int32)
    large = np.minimum(large, half - 1)
    return ret + np.where(n < max_exact, n, large)


def _bias_tables(past):
    out = {}
    k = np.arange(128)[:, None]
    q = np.arange(128)[None, :]

    def mk(rel, allowed):
        b = _t5_bucket(rel)
        oh = np.zeros((rel.shape[0], 32, rel.shape[1]), np.float32)
        for bb in range(32):
            oh[:, bb, :] = ((b == bb) & allowed)
        neg = np.where(allowed, 0.0, NEG).astype(np.float32)
        return oh, neg
    out["diag"] = mk(k - q, (k // 64) <= (q // 64))
    out["prev"] = mk(k - 128 - q, np.ones((128, 128), bool))
    qs = np.arange(32)[None, :]
    out["slast"] = mk((past - 128 + k) - (past + qs), np.ones((128, 32), bool))
    ks = np.arange(32)[:, None]
    out["snew"] = mk(ks - qs, np.ones((32, 32), bool))
    return out


def build(SEQ, PAST, L):
    NTP = SEQ // 512
    NBLK = SEQ // 128
    NS = 8
    NT = SEQ + 32 * NS
    PB = PAST // 128
    GH = 32
    GW = GH + SEQ + NS * (GH + 32)
    nc = bass.Bass("TRN2", target_bir_lowering=False)

    def din(name, shape, dt=F32):
        return nc.dram_tensor(name, list(shape), dt, kind="ExternalInput").ap()

    def dout(name, shape):
        return nc.dram_tensor(name, list(shape), F32, kind="ExternalOutput").ap()

    xT_p = din("xT_p", [128, DC, SEQ])
    xT_s = din("xT_s", [128, DC, 32 * NS])
    kcT = din("kcT", [L, NS, H, 128, PAST])
    vc = din("vc", [L, NS, H, 128, PB, 128])
    scT = din("scT", [L, NS, 128, DC, HALO])
    wA = din("wA", [L, 128, DC * 5120])
    wD = din("wD", [L, 128, DC * 5120])
    wE1 = din("wE1", [L, 128, DC * 2 * DFF])
    wE2 = din("wE2", [L, 128, FC * D])
    gvec = din("gvec", [128, 5 * L * DC + DC])
    cdw = din("cdw", [128, L * DC * CW])
    subg = din("subg", [128, L * 128])
    lamv = din("lamv", [128, 4 * L * 64])
    rbv = din("rbv", [128, 32 * H])
    identf = din("identf", [128, 128])
    oh_in = {k: din("oh_" + k, list(v[0].shape)) for k, v in _bias_tables(PAST).items()}
    ng_in = {k: din("ng_" + k, list(v[1].shape)) for k, v in _bias_tables(PAST).items()}

    y_p = dout("y_p", [SEQ, D])
    y_s = dout("y_s", [NS, 32, D])
    k_p = dout("k_p", [L, SEQ, D])
    v_p = dout("v_p", [L, SEQ, D])
    c_p = dout("c_p", [L, HALO, D])
    k_s = dout("k_s", [L, NS, 32, D])
    v_s = dout("v_s", [L, NS, 32, D])
    c_s = dout("c_s", [L, NS, HALO, D])

    X0 = nc.dram_tensor("X0", [128, DC, NT], F32).ap()
    X1 = nc.dram_tensor("X1", [128, DC, NT], F32).ap()
    KT = nc.dram_tensor("KT", [H, 128, NT], BF16).ap()
    QT = nc.dram_tensor("QT", [H, 128, NT], BF16).ap()
    OT = nc.dram_tensor("OT", [128, H, NT], BF16).ap()
    VB = nc.dram_tensor("VB", [H, 128, NBLK + NS, 128], BF16).ap()
    G = nc.dram_tensor("G", [128, DC, GW], BF16).ap()

    tiles = [(j * 512, 512) for j in range(NTP)] + [(SEQ + 32 * s_, 32) for s_ in range(NS)]

    def gcol(t0):
        return t0 if t0 < SEQ else SEQ + GH + ((t0 - SEQ) // 32) * (GH + 32)

    with contextlib.ExitStack() as st:
        fw = FW(nc, st)
        psum = lambda name, shape, dt=F32: st.enter_context(nc.psum_tensor(name, list(shape), dt))
        PS2 = [psum("ps2_%d" % i, [128, 2, 512]) for i in range(2)]
        PS2b = [Buf() for _ in range(2)]
        PO = [psum("po_%d" % i, [128, 512]) for i in range(3)]
        POb = [Buf() for _ in range(3)]
        PTt = psum("ptb", [128, 1024], BF16)
        PTb = Buf()
        mm_banks = [(PS2[0], 0), (PS2[0], 1), (PS2[1], 0), (PS2[1], 1)]
        mm_bufs = [Buf() for _ in range(4)]
        mm_i = [0]

        def mmbank():
            i = mm_i[0] % 4
            mm_i[0] += 1
            t, m = mm_banks[i]
            return t[:, m, :], mm_bufs[i]

        def reset_psum_bufs():
            for b in PS2b + POb + [PTb] + mm_bufs:
                b.w = {}
                b.r = {}

        pst = st
        sbp = lambda name, shape, dt: pst.enter_context(nc.sbuf_tensor(name, list(shape), dt))
        ident_f = sbp("ident_f", [128, 128], F32)
        ident_b = sbp("ident_b", [128, 128], BF16)
        ones_b = sbp("ones_b", [128, 128], BF16)
        gv = sbp("gv", [128, 5 * L * DC + DC], F32)
        cdw_s = sbp("cdw_s", [128, L * DC * CW], F32)
        subg_s = sbp("subg_s", [128, L * 128], F32)
        lam_s = sbp("lam_s", [128, 4 * L], F32)
        rb_s = sbp("rb_s", [128, 32 * H], F32)
        eps_s = sbp("eps_s", [128, 3], F32)
        bm = {k: sbp("bm_" + k, [v[1].shape[0], H, v[1].shape[1]], F32) for k, v in _bias_tables(PAST).items()}
        tabs = _bias_tables(PAST)

        def G_(kind, l, dc):
            base = {"mix": 0, "ffn": 1, "lng": 2, "lnb": 3, "dwb": 4}[kind] * L * DC
            c = base + l * DC + dc
            return gv[:, c:c + 1]

        with contextlib.ExitStack() as ph:
            fw.begin(ph)
            reset_psum_bufs()
            cb = Buf()
            s0 = fw.pdsem()
            loads = [(ident_f, identf), (gv, gvec), (cdw_s, cdw), (subg_s, subg), (rb_s, rbv)]
            lamt = fw.sb("lamt", [128, 4 * L * 64], F32)
            loads.append((lamt, lamv))
            for tt, src in loads:
                fw.op("sp", lambda e, tt=tt, src=src: e.dma_start(out=tt[:], in_=src), writes=[cb], dsem=s0)
            ohts = {}
            for key in ("diag", "prev", "slast", "snew"):
                oh, neg = tabs[key]
                kk, _, qq = oh.shape
                oht = fw.sb("oht_" + key, [kk, 32, qq], F32)
                ngt = fw.sb("ngt_" + key, [kk, qq], F32)
                ohts[key] = (oht, ngt)
                fw.op("sp", lambda e, oht=oht, key=key: e.dma_start(out=oht[:], in_=oh_in[key]), writes=[cb], dsem=s0)
                fw.op("sp", lambda e, ngt=ngt, key=key: e.dma_start(out=ngt[:], in_=ng_in[key]), writes=[cb], dsem=s0)
            fw.barrier()
            cb = Buf()
            _sub = int(os.environ.get("KSUB", "9"))
            fw.op("dve", lambda e: e.memset(ones_b[:], 1.0), writes=[cb])
            fw.op("dve", lambda e: e.tensor_copy(out=ident_b[:], in_=ident_f[:]), reads=[cb], writes=[cb])
            fw.op("dve", lambda e: e.memset(eps_s[:, 0:1], 1e-6), writes=[cb])
            fw.op("dve", lambda e: e.memset(eps_s[:, 1:3], 1e-5), writes=[cb])
            if _sub < 2:
                fw.end()
                return nc
            lprod = fw.sb("lprod", [128, 2 * L, 64], F32)
            lsum = fw.sb("lsum", [128, 2 * L], F32)
            lv = lamt[:].rearrange("p (a l d) -> p a l d", a=4, l=L)
            fw.op("dve", lambda e: e.tensor_tensor(out=lprod[:, 0:L, :], in0=lv[:, 0], in1=lv[:, 1], op=ALU.mult), reads=[cb], writes=[cb])
            fw.op("dve", lambda e: e.tensor_tensor(out=lprod[:, L:2 * L, :], in0=lv[:, 2], in1=lv[:, 3], op=ALU.mult), reads=[cb], writes=[cb])
            fw.op("dve", lambda e: e.tensor_reduce(out=lsum[:], in_=lprod[:], axis=AX.X, op=ALU.add), reads=[cb], writes=[cb])
            fw.op("act", lambda e: e.activation(out=lsum[:], in_=lsum[:], func=AF.Exp), reads=[cb], writes=[cb])
            for l in range(L):
                li = 0.8 - 0.6 * math.exp(-0.3 * l)
                fw.op("dve", lambda e, l=l, li=li: e.scalar_tensor_tensor(
                    out=lam_s[:, 4 * l + 2:4 * l + 3], in0=lsum[:, l:l + 1], scalar=li, in1=lsum[:, L + l:L + l + 1],
                    op0=ALU.add, op1=ALU.subtract), reads=[cb], writes=[cb])
                fw.op("dve", lambda e, l=l: e.tensor_scalar(
                    out=lam_s[:, 4 * l + 3:4 * l + 4], in0=lam_s[:, 4 * l + 2:4 * l + 3], scalar1=-1.0, scalar2=None,
                    op0=ALU.mult), reads=[cb], writes=[cb])
                fw.op("dve", lambda e, l=l, li=li: e.tensor_scalar(
                    out=subg_s[:, l * 128:(l + 1) * 128], in0=subg_s[:, l * 128:(l + 1) * 128], scalar1=1.0 - li,
                    scalar2=None, op0=ALU.mult), reads=[cb], writes=[cb])
            if _sub < 3:
                fw.end()
                return nc
            for key in ("diag", "prev", "slast", "snew"):
                oh, neg = tabs[key]
                kk, _, qq = oh.shape
                oht, ngt = ohts[key]
                used = [b for b in range(32) if oh[:, b, :].any()]
                for h in range(H):
                    first = True
                    for b in used:
                        src = ngt[:] if first else bm[key][:, h, :]
                        fw.op("dve", lambda e, key=key, h=h, b=b, src=src, oht=oht, kk=kk: e.scalar_tensor_tensor(
                            out=bm[key][:, h, :], in0=oht[:, b, :], scalar=rb_s[0:kk, b * H + h:b * H + h + 1], in1=src,
                            op0=ALU.mult, op1=ALU.add), reads=[cb], writes=[cb])
                        first = False
            if _sub < 4:
                fw.end()
                return nc
            zt = fw.sb("zt", [128, DC, GH], BF16)
            fw.op("dve", lambda e: e.memset(zt[:], 0.0), writes=[cb])
            fw.op("pool", lambda e: e.dma_start(out=G[:, :, 0:GH], in_=zt[:]), reads=[cb], dsem=fw.pdsem(True))
            fw.end()

        def load_w(wt, src, ncols, wb, ws):
            step = 4096
            for c0 in range(0, ncols, step):
                c1 = min(ncols, c0 + step)
                fw.op("pool", lambda e, c0=c0, c1=c1: e.dma_start(out=wt[:, c0:c1], in_=src[:, c0:c1]),
                      writes=[wb], dsem=ws)

        def rmsnorm(xt, xb, n, gsel, ht, hb, sq, sqb, rs, rsb):
            fw.op("act", lambda e: e.activation(out=sq[:, :, 0:n], in_=xt[:, :, 0:n], func=AF.Square), reads=[xb], writes=[sqb])
            pt, pb = mmbank()

            def mm(e):
                r = None
                for dc in range(DC):
                    r = e.matmul(pt[:, 0:n], lhsT=ones_b[:], rhs=sq[:, dc, 0:n], start=(dc == 0), stop=(dc == DC - 1))
                return r
            fw.op("pe", mm, reads=[sqb], writes=[pb])
            fw.op("act", lambda e: e.activation(out=rs[:, 0:n], in_=pt[:, 0:n], func=AF.Sqrt, scale=1.0 / D, bias=eps_s[:, 0:1]),
                  reads=[pb], writes=[rsb])
            fw.op("dve", lambda e: e.reciprocal(out=rs[:, 0:n], in_=rs[:, 0:n]), reads=[rsb], writes=[rsb])
            for dc in range(DC):
                fw.op("dve", lambda e, dc=dc: e.scalar_tensor_tensor(
                    out=ht[:, dc, 0:n], in0=xt[:, dc, 0:n], scalar=gsel(dc), in1=rs[:, 0:n], op0=ALU.mult, op1=ALU.mult),
                    reads=[xb, rsb], writes=[hb])

        def xsrc(l, which, t0, n):
            if which == 0:
                if l == 0:
                    return xT_p[:, :, t0:t0 + n] if t0 < SEQ else xT_s[:, :, t0 - SEQ:t0 - SEQ + n]
                return X0[:, :, t0:t0 + n]
            return X1[:, :, t0:t0 + n]

        _stop = os.environ.get("KSTOP", "")
        for l in range(L):
            if _stop == "setup":
                break
            with contextlib.ExitStack() as ph:
                fw.begin(ph)
                reset_psum_bufs()
                W = fw.sb("wa", [128, DC * 5120], BF16)
                Wb = Buf()
                load_w(W, wA[l], DC * 5120, Wb, fw.pdsem(True))
                Wv = W[:].rearrange("p (d c) -> p d c", d=DC)
                xr = Ring(fw, "ax", [128, DC, 512], F32, 2)
                hr = Ring(fw, "ah", [128, DC, 512], BF16, 1, dma=False)
                sq = fw.sb("asq", [128, DC, 512], BF16); sqb = Buf()
                rs = fw.sb("ars", [128, 512], F32); rsb = Buf()
                kst = Ring(fw, "akst", [128, D], F32, 2)
                vst = Ring(fw, "avst", [128, D], F32, 2)
                vbr = Ring(fw, "avb", [128, H, 128], BF16, 2)
                ktr = Ring(fw, "akt", [128, H, 512], BF16, 1)
                qtr = Ring(fw, "aqt", [128, H, 512], BF16, 1)
                gtr = Ring(fw, "agt", [128, DC, 512], BF16, 1)
                sgr = Ring(fw, "asg", [128, 512], F32, 2, dma=False)
                cst = Ring(fw, "acst", [HALO, D], F32, 1)
                sct = fw.sb("asct", [128, DC, GH], BF16); scb = Buf(); scs = fw.pdsem(True)
                fw.op("dve", lambda e: e.memset(sct[:], 0.0), writes=[scb])
                for s_ in range(NS):
                    g0 = SEQ + GH + s_ * (GH + 32)
                    fw.op("pool", lambda e, s_=s_: e.dma_start(out=sct[:, :, GH - HALO:GH], in_=scT[l, s_]), writes=[scb], dsem=scs)
                    fw.op("pool", lambda e, g0=g0: e.dma_start(out=G[:, :, g0:g0 + GH], in_=sct[:]), reads=[scb], dsem=scs)

                def tileA(t0, n):
                    samp = t0 >= SEQ
                    sidx = (t0 - SEQ) // 32 if samp else 0
                    xt, xb, xs = xr.next()
                    fw.op("sp", lambda e, xt=xt, t0=t0, n=n: e.dma_start(out=xt[:, :, 0:n], in_=xsrc(l, 0, t0, n)), writes=[xb], dsem=xs)
                    ht, hb, _ = hr.next()
                    rmsnorm(xt, xb, n, lambda dc: G_("mix", l, dc), ht, hb, sq, sqb, rs, rsb)
                    nsb = max(1, n // 128)
                    mtok = min(n, 128)
                    _ka = int(os.environ.get("KB", "9"))
                    if _ka < 2:
                        return
                    for kind, c0, ring, outp, outs in (("k", 0, kst, k_p, k_s), ("v", 1024, vst, v_p, v_s)):
                        for sbi in range(nsb):
                            stt, stb, sts = ring.next()
                            if kind == "v":
                                vbt, vbb, vbs = vbr.next()
                            for half in range(2):
                                pt, pb = mmbank()

                                def mm(e, pt=pt, sbi=sbi, half=half, c0=c0, ht=ht):
                                    r = None
                                    for dc in range(DC):
                                        r = e.matmul(pt[0:mtok, :], lhsT=ht[:, dc, sbi * 128:sbi * 128 + mtok],
                                                     rhs=Wv[:, dc, c0 + half * 512:c0 + half * 512 + 512],
                                                     start=(dc == 0), stop=(dc == DC - 1))
                                    return r
                                fw.op("pe", mm, reads=[hb, Wb], writes=[pb])
                                fw.op("act", lambda e, pt=pt, stt=stt, half=half: e.copy(out=stt[0:mtok, half * 512:(half + 1) * 512], in_=pt[0:mtok, :]),
                                      writes=[stb, pb])
                                if kind == "v":
                                    fw.op("dve", lambda e, pt=pt, vbt=vbt, half=half: e.tensor_copy(
                                        out=vbt[0:mtok, half * 4:(half + 1) * 4, 0:128],
                                        in_=pt[0:mtok, :].rearrange("p (h e) -> p h e", h=4)), writes=[vbb, pb])
                            dst = (outs[l, sidx, 0:mtok, :] if samp else outp[l, t0 + sbi * 128:t0 + sbi * 128 + 128, :])
                            fw.op("sp", lambda e, dst=dst, stt=stt: e.dma_start(out=dst, in_=stt[0:mtok, :]), reads=[stb], dsem=sts)
                            if kind == "v" and os.environ.get("KV") != "novb":
                                blk = (NBLK + sidx) if samp else (t0 // 128 + sbi)
                                fw.op("sp", lambda e, blk=blk, vbt=vbt: e.dma_start(
                                    out=VB[:, 0:mtok, blk, :].rearrange("h p e -> p h e"), in_=vbt[0:mtok, :, :]), reads=[vbb], dsem=vbs)
                    if _ka < 3:
                        return
                    for kind, c0, ring, dstT, scl in (("kt", 0, ktr, KT, 1.0), ("qt", 4096, qtr, QT, 0.125)):
                        stt, stb, sts = ring.next()
                        for hc in range(H):
                            pt, pb = mmbank()

                            def mm(e, pt=pt, hc=hc, c0=c0, ht=ht):
                                r = None
                                for dc in range(DC):
                                    r = e.matmul(pt[:, 0:n], lhsT=Wv[:, dc, c0 + hc * 128:c0 + hc * 128 + 128], rhs=ht[:, dc, 0:n],
                                                 start=(dc == 0), stop=(dc == DC - 1))
                                return r
                            fw.op("pe", mm, reads=[hb, Wb], writes=[pb])
                            fw.op("act", lambda e, pt=pt, stt=stt, hc=hc, scl=scl: e.mul(out=stt[:, hc, 0:n], in_=pt[:, 0:n], mul=scl),
                                  reads=[pb], writes=[stb])
                        fw.op("sp", lambda e, stt=stt, dstT=dstT: e.dma_start(
                            out=dstT[:, :, t0:t0 + n].rearrange("h p t -> p h t"), in_=stt[:, :, 0:n]), reads=[stb], dsem=sts)
                    if _ka < 4:
                        return
                    gt, gb, gs = gtr.next()
                    for c in range(DC):
                        pv, pvb = mmbank()
                        pg, pgb = mmbank()
                        for pt, pb, c0 in ((pv, pvb, 2048), (pg, pgb, 3072)):
                            def mm(e, pt=pt, c=c, c0=c0, ht=ht):
                                r = None
                                for dc in range(DC):
                                    r = e.matmul(pt[:, 0:n], lhsT=Wv[:, dc, c0 + c * 128:c0 + c * 128 + 128], rhs=ht[:, dc, 0:n],
                                                 start=(dc == 0), stop=(dc == DC - 1))
                                return r
                            fw.op("pe", mm, reads=[hb, Wb], writes=[pb])
                        sg, sgb, _ = sgr.next()
                        fw.op("act", lambda e, pg=pg, sg=sg: e.activation(out=sg[:, 0:n], in_=pg[:, 0:n], func=AF.Sigmoid), reads=[pgb], writes=[sgb])
                        fw.op("dve", lambda e, pv=pv, sg=sg, gt=gt, c=c: e.tensor_tensor(out=gt[:, c, 0:n], in0=pv[:, 0:n], in1=sg[:, 0:n], op=ALU.mult),
                              reads=[pvb, sgb], writes=[gb])
                    fw.op("sp", lambda e, gt=gt: e.dma_start(out=G[:, :, gcol(t0) + GH:gcol(t0) + GH + n], in_=gt[:, :, 0:n]), reads=[gb], dsem=gs)
                    if _ka < 5:
                        return
                    if samp or t0 + n == SEQ:
                        ct, ctb, cs_ = cst.next()
                        for half in range(2):
                            pt, pb = mmbank()

                            def mm(e, pt=pt, half=half, gt=gt):
                                r = None
                                for c4 in range(4):
                                    c = half * 4 + c4
                                    r = e.matmul(pt[0:HALO, c4 * 128:(c4 + 1) * 128], lhsT=gt[:, c, n - HALO:n], rhs=ident_b[:],
                                                 start=True, stop=True)
                                return r
                            fw.op("pe", mm, reads=[gb], writes=[pb])
                            fw.op("act", lambda e, pt=pt, ct=ct, half=half: e.copy(out=ct[:, half * 512:(half + 1) * 512], in_=pt[0:HALO, :]),
                                  reads=[pb], writes=[ctb])
                        dst = c_s[l, sidx] if samp else c_p[l]
                        fw.op("sp", lambda e, dst=dst, ct=ct: e.dma_start(out=dst, in_=ct[:]), reads=[ctb], dsem=cs_)
                for (t0, n) in tiles:
                    if os.environ.get("KA") == "nosample" and t0 >= SEQ:
                        continue
                    if os.environ.get("KA") == "onlysample" and t0 < SEQ:
                        continue
                    tileA(t0, n)
                fw.end()
            if _stop == "A%d" % l:
                break

            with contextlib.ExitStack() as ph:
                fw.begin(ph)
                reset_psum_bufs()
                KCH = 8
                kr = Ring(fw, "ck", [128, KCH * 128], BF16, 3)
                vr = Ring(fw, "cv", [128, KCH, 129], BF16, 3)
                for i in range(3):
                    fw.op("dve", lambda e, i=i: e.memset(vr.t[i][:, :, 128:129], 1.0), writes=[vr.b[i]])
                qr = Ring(fw, "cq", [128, 512], BF16, 2)
                pr = Ring(fw, "cp", [128, 2, 512], BF16, 3, dma=False)
                tr = Ring(fw, "ctmp", [128, 2, 128], F32, 2, dma=False)
                otr = Ring(fw, "cot", [128, 512], BF16, 2)
                fo = fw.sb("cfo", [128, 128], F32); fob = Buf()
                ft = fw.sb("cft", [128, 128], F32); ftb = Buf()
                fs = fw.sb("cfs", [128, 8], F32); fsb = Buf()
                fb = fw.sb("cfb", [128, 128], BF16); fbb = Buf()
                kcs = fw.sb("ckc", [128, PAST], BF16); kcb = Buf(); kcsem = fw.pdsem(True)
                vcs = fw.sb("cvc", [128, PB, 129], BF16); vcb = Buf(); vcsem = fw.pdsem(True)
                fw.op("dve", lambda e: e.memset(vcs[:, :, 128:129], 1.0), writes=[vcb])
                ssl = [0]

                def acc(idx):
                    return PO[idx // 3][:, (idx % 3) * 129:(idx % 3) * 129 + 129], POb[idx // 3]

                def finalize(h, np_, idx1, idx2, ott, otb, col):
                    a1, b1 = acc(idx1)
                    a2, b2 = acc(idx2)
                    fw.op("dve", lambda e: e.reciprocal(out=fs[0:np_, 0:1], in_=a1[0:np_, 128:129]), reads=[b1], writes=[fsb])
                    fw.op("dve", lambda e: e.reciprocal(out=fs[0:np_, 1:2], in_=a2[0:np_, 128:129]), reads=[b2], writes=[fsb])
                    fw.op("dve", lambda e: e.tensor_tensor(out=fs[0:np_, 1:2], in0=fs[0:np_, 1:2], in1=lam_s[0:np_, 4 * l + 3:4 * l + 4], op=ALU.mult),
                          reads=[fsb], writes=[fsb])
                    fw.op("dve", lambda e: e.tensor_scalar(out=ft[0:np_, :], in0=a2[0:np_, 0:128], scalar1=fs[0:np_, 1:2], scalar2=None, op0=ALU.mult),
                          reads=[b2, fsb], writes=[ftb])
                    fw.op("dve", lambda e: e.scalar_tensor_tensor(out=fo[0:np_, :], in0=a1[0:np_, 0:128], scalar=fs[0:np_, 0:1], in1=ft[0:np_, :],
                                                                  op0=ALU.mult, op1=ALU.add), reads=[b1, fsb, ftb], writes=[fob])
                    fw.op("dve", lambda e: e.memset(fs[0:np_, 2:3], 0.0), reads=[fsb], writes=[fsb])
                    fw.op("act", lambda e: e.activation(out=ft[0:np_, :], in_=fo[0:np_, :], func=AF.Square, accum_out=fs[0:np_, 2:3]),
                          reads=[fob], writes=[ftb, fsb])
                    fw.op("act", lambda e: e.activation(out=fs[0:np_, 3:4], in_=fs[0:np_, 2:3], func=AF.Sqrt, scale=1.0 / 128, bias=eps_s[0:np_, 2:3]),
                          reads=[fsb], writes=[fsb])
                    fw.op("dve", lambda e: e.reciprocal(out=fs[0:np_, 3:4], in_=fs[0:np_, 3:4]), reads=[fsb], writes=[fsb])
                    fw.op("dve", lambda e: e.scalar_tensor_tensor(out=fb[0:np_, :], in0=fo[0:np_, :], scalar=fs[0:np_, 3:4],
                                                                  in1=subg_s[0:np_, l * 128:(l + 1) * 128], op0=ALU.mult, op1=ALU.mult),
                          reads=[fob, fsb], writes=[fbb])
                    fw.op("pe", lambda e: e.transpose(out=PTt[:, 0:np_], in_=fb[0:np_, :], identity=ident_b[0:np_, 0:np_]), reads=[fbb], writes=[PTb])
                    fw.op("act", lambda e: e.copy(out=ott[:, col:col + np_], in_=PTt[:, 0:np_]), reads=[PTb], writes=[otb])

                def headC(h):
                    c15 = rb_s[:, 15 * H + h:15 * H + h + 1]
                    def groupC(g):
                        qt, qb, qs = qr.next()
                        fw.op("sp", lambda e, qt=qt, g=g: e.dma_start(out=qt[:], in_=QT[h, :, g * 512:(g + 1) * 512]), writes=[qb], dsem=qs)
                        ott, otb, ots = otr.next()
                        nkb = 4 * g + 4
                        pend = None
                        for kc0 in range(0, nkb, KCH):
                            nb = min(KCH, nkb - kc0)
                            kt, kb_, ks = kr.next()
                            vt, vb_, vs = vr.next()
                            fw.op("sp", lambda e, kt=kt, kc0=kc0, nb=nb: e.dma_start(out=kt[:, 0:nb * 128], in_=KT[h, :, kc0 * 128:(kc0 + nb) * 128]),
                                  writes=[kb_], dsem=ks)
                            fw.op("sp", lambda e, vt=vt, kc0=kc0, nb=nb: e.dma_start(out=vt[:, 0:nb, 0:128], in_=VB[h, :, kc0:kc0 + nb, :]),
                                  writes=[vb_], dsem=vs)
                            for kl in range(nb):
                                kb = kc0 + kl
                                ibmin = max(0, kb - 4 * g)
                                col0 = ibmin * 128
                                si = ssl[0] % 2
                                ssl[0] += 1
                                S, Sb = PS2[si], PS2b[si]

                                def qk(e, S=S, kt=kt, kl=kl, qt=qt, col0=col0):
                                    r = None
                                    for m in range(2):
                                        r = e.matmul(S[:, m, col0:512], lhsT=kt[m * 64:(m + 1) * 64, kl * 128:(kl + 1) * 128],
                                                     rhs=qt[m * 64:(m + 1) * 64, col0:512], start=True, stop=True)
                                    return r
                                fw.op("pe", qk, reads=[kb_, qb], writes=[Sb])
                                if pend is not None:
                                    pend()
                                pt_, ptb, _ = pr.next()
                                ibp = max(0, kb + 2 - 4 * g)
                                if ibp <= 3:
                                    fw.op("act", lambda e, S=S, pt_=pt_, ibp=ibp: e.activation(
                                        out=pt_[:, :, ibp * 128:512], in_=S[:, :, ibp * 128:512], func=AF.Exp, bias=c15, scale=1.0),
                                        writes=[ptb, Sb])
                                for ib, key in ((kb - 4 * g, "diag"), (kb + 1 - 4 * g, "prev")):
                                    if 0 <= ib <= 3:
                                        tt, ttb, _ = tr.next()
                                        for m in range(2):
                                            fw.op("dve", lambda e, S=S, tt=tt, ib=ib, key=key, m=m: e.tensor_tensor(
                                                out=tt[:, m, :], in0=S[:, m, ib * 128:(ib + 1) * 128], in1=bm[key][:, h, :], op=ALU.add),
                                                writes=[ttb, Sb])
                                        fw.op("act", lambda e, tt=tt, pt_=pt_, ib=ib: e.activation(
                                            out=pt_[:, :, ib * 128:(ib + 1) * 128], in_=tt[:, :, :], func=AF.Exp), reads=[ttb], writes=[ptb])

                                def pv(pt_=pt_, ptb=ptb, vt=vt, vb_=vb_, kl=kl, kb=kb, ibmin=ibmin, g=g):
                                    for ib in range(ibmin, 4):
                                        for m in range(2):
                                            a, ab = acc(ib * 2 + m)
                                            fw.op("pe", lambda e, a=a, pt_=pt_, ib=ib, m=m, vt=vt, kl=kl, kb=kb, g=g: e.matmul(
                                                a, lhsT=pt_[:, m, ib * 128:(ib + 1) * 128], rhs=vt[:, kl, :],
                                                start=(kb == 0 and (ib * 2 + m) % 3 == 0),
                                                stop=((ib * 2 + m, kb - 4 * g) in ((2, 1), (5, 2), (7, 3)))), reads=[ptb, vb_], writes=[ab])
                                pend = pv
                        pend()
                        for ib in range(4):
                            finalize(h, 128, ib * 2, ib * 2 + 1, ott, otb, ib * 128)
                        fw.op("sp", lambda e, ott=ott, g=g: e.dma_start(out=OT[:, h, g * 512:(g + 1) * 512], in_=ott[:]), reads=[otb], dsem=ots)
                    for g in range(NTP):
                        groupC(g)
                    def sampleC(sidx):
                        sq0 = SEQ + 32 * sidx
                        qt, qb, qs = qr.next()
                        fw.op("sp", lambda e, qt=qt: e.dma_start(out=qt[:, 0:32], in_=QT[h, :, sq0:sq0 + 32]), writes=[qb], dsem=qs)
                        fw.op("pool", lambda e: e.dma_start(out=kcs[:], in_=kcT[l, sidx, h]), writes=[kcb], dsem=kcsem)
                        fw.op("pool", lambda e: e.dma_start(out=vcs[:, :, 0:128], in_=vc[l, sidx, h]), writes=[vcb], dsem=vcsem)
                        kt, kb_, ks = kr.next()
                        vt, vb_, vs = vr.next()
                        fw.op("sp", lambda e, kt=kt: e.dma_start(out=kt[:, 0:32], in_=KT[h, :, sq0:sq0 + 32]), writes=[kb_], dsem=ks)
                        fw.op("sp", lambda e, vt=vt: e.dma_start(out=vt[0:32, 0, 0:128], in_=VB[h, 0:32, NBLK + sidx, :]), writes=[vb_], dsem=vs)
                        ott, otb, ots = otr.next()
                        a1, ab1 = acc(0)
                        a2, ab2 = acc(1)
                        for k0 in range(0, PB + 1, 16):
                            kbs = list(range(k0, min(PB + 1, k0 + 16)))
                            si = ssl[0] % 2
                            ssl[0] += 1
                            S, Sb = PS2[si], PS2b[si]
                            pt_, ptb, _ = pr.next()

                            def qk(e, S=S, kbs=kbs, qt=qt, kt=kt):
                                r = None
                                for j, kb in enumerate(kbs):
                                    for m in range(2):
                                        if kb < PB:
                                            r = e.matmul(S[:, m, j * 32:(j + 1) * 32], lhsT=kcs[m * 64:(m + 1) * 64, kb * 128:(kb + 1) * 128],
                                                         rhs=qt[m * 64:(m + 1) * 64, 0:32], start=True, stop=True)
                                        else:
                                            r = e.matmul(S[0:32, m, j * 32:(j + 1) * 32], lhsT=kt[m * 64:(m + 1) * 64, 0:32],
                                                         rhs=qt[m * 64:(m + 1) * 64, 0:32], start=True, stop=True)
                                return r
                            fw.op("pe", qk, reads=[kcb, kb_, qb], writes=[Sb])
                            plain = [j for j, kb in enumerate(kbs) if kb < PB - 1]
                            if plain:
                                npl = len(plain)
                                fw.op("act", lambda e, S=S, pt_=pt_, npl=npl: e.activation(
                                    out=pt_[:, :, 0:npl * 32], in_=S[:, :, 0:npl * 32], func=AF.Exp, bias=c15, scale=1.0), writes=[ptb, Sb])
                            for j, kb in enumerate(kbs):
                                if kb >= PB - 1:
                                    key, kk = ("slast", 128) if kb == PB - 1 else ("snew", 32)
                                    tt, ttb, _ = tr.next()
                                    for m in range(2):
                                        fw.op("dve", lambda e, S=S, tt=tt, j=j, key=key, kk=kk, m=m: e.tensor_tensor(
                                            out=tt[0:kk, m, 0:32], in0=S[0:kk, m, j * 32:(j + 1) * 32], in1=bm[key][:, h, :], op=ALU.add),
                                            writes=[ttb, Sb])
                                    fw.op("act", lambda e, tt=tt, pt_=pt_, j=j, kk=kk: e.activation(
                                        out=pt_[0:kk, :, j * 32:(j + 1) * 32], in_=tt[0:kk, :, 0:32], func=AF.Exp), reads=[ttb], writes=[ptb])
                            for j, kb in enumerate(kbs):
                                for m, (a, ab) in enumerate(((a1, ab1), (a2, ab2))):
                                    if kb < PB:
                                        fw.op("pe", lambda e, a=a, pt_=pt_, j=j, m=m, kb=kb: e.matmul(
                                            a[0:32, :], lhsT=pt_[:, m, j * 32:(j + 1) * 32], rhs=vcs[:, kb, :], start=(kb == 0 and m == 0), stop=False),
                                            reads=[ptb, vcb], writes=[ab])
                                    else:
                                        fw.op("pe", lambda e, a=a, pt_=pt_, j=j, m=m, vt=vt: e.matmul(
                                            a[0:32, :], lhsT=pt_[0:32, m, j * 32:(j + 1) * 32], rhs=vt[0:32, 0, :], start=False, stop=(m == 1)),
                                            reads=[ptb, vb_], writes=[ab])
                        finalize(h, 32, 0, 1, ott, otb, 0)
                        fw.op("sp", lambda e, ott=ott: e.dma_start(out=OT[:, h, sq0:sq0 + 32], in_=ott[:, 0:32]), reads=[otb], dsem=ots)
                    for s_ in range(NS):
                        sampleC(s_)
                for h in range(H):
                    headC(h)
                fw.end()
            if _stop == "C%d" % l:
                break

            with contextlib.ExitStack() as ph:
                fw.begin(ph)
                reset_psum_bufs()
                W = fw.sb("wd", [128, DC * 5120], BF16)
                Wb = Buf()
                load_w(W, wD[l], DC * 5120, Wb, fw.pdsem(True))
                Wv = W[:].rearrange("p (d c) -> p d c", d=DC)
                xr = Ring(fw, "dx", [128, DC, 512], F32, 1)
                hr = Ring(fw, "dh", [128, DC, 512], BF16, 1, dma=False)
                sq = fw.sb("dsq", [128, DC, 512], BF16); sqb = Buf()
                rs = fw.sb("drs", [128, 512], F32); rsb = Buf()
                glr = Ring(fw, "dgl", [128, DC, 512 + GH], BF16, 1)
                oir = Ring(fw, "doi", [128, H, 512], BF16, 1)
                dw = fw.sb("ddw", [128, DC, 512], F32); dwb = [Buf() for _ in range(DC)]
                st1 = fw.sb("dst1", [128, 512], F32); st1b = Buf()
                st2 = fw.sb("dst2", [128, 512], F32); st2b = Buf()
                ct = fw.sb("dct", [128, DC, 512], BF16); ctb = Buf()
                mt = fw.sb("dmt", [128, DC, 512], BF16); mtb = Buf()
                ga = Ring(fw, "dga", [128, 512], F32, 2, dma=False)
                gbm = Ring(fw, "dgb", [128, 512], F32, 2, dma=False)
                sg = Ring(fw, "dsg", [128, 512], F32, 2, dma=False)

                def proj(c0, j, src, srcb, n):
                    pt, pb = mmbank()

                    def mm(e):
                        r = None
                        for dc in range(DC):
                            r = e.matmul(pt[:, 0:n], lhsT=Wv[:, dc, c0 + j * 128:c0 + j * 128 + 128], rhs=src[:, dc, 0:n],
                                         start=(dc == 0), stop=(dc == DC - 1))
                        return r
                    fw.op("pe", mm, reads=[srcb, Wb], writes=[pb])
                    return pt, pb

                def tileD(t0, n):
                    xt, xb, xs = xr.next()
                    fw.op("sp", lambda e, xt=xt, t0=t0, n=n: e.dma_start(out=xt[:, :, 0:n], in_=xsrc(l, 0, t0, n)), writes=[xb], dsem=xs)
                    glt, glb, gls = glr.next()
                    fw.op("sp", lambda e, glt=glt, t0=t0, n=n: e.dma_start(out=glt[:, :, 0:n + GH], in_=G[:, :, gcol(t0):gcol(t0) + n + GH]),
                          writes=[glb], dsem=gls)
                    oit, oib, ois = oir.next()
                    fw.op("sp", lambda e, oit=oit, t0=t0, n=n: e.dma_start(out=oit[:, :, 0:n], in_=OT[:, :, t0:t0 + n]), writes=[oib], dsem=ois)
                    ht, hb, _ = hr.next()
                    rmsnorm(xt, xb, n, lambda dc: G_("mix", l, dc), ht, hb, sq, sqb, rs, rsb)
                    for c in range(DC):
                        wof = (l * DC + c) * CW
                        fw.op("dve", lambda e, c=c, wof=wof, glt=glt: e.tensor_scalar(
                            out=dw[:, c, 0:n], in0=glt[:, c, GH - HALO:GH - HALO + n], scalar1=cdw_s[:, wof:wof + 1], scalar2=G_("dwb", l, c),
                            op0=ALU.mult, op1=ALU.add), reads=[glb], writes=[dwb[c]])
                        for j in range(1, CW):
                            fw.op("dve", lambda e, c=c, wof=wof, j=j, glt=glt: e.scalar_tensor_tensor(
                                out=dw[:, c, 0:n], in0=glt[:, c, GH - HALO + j:GH - HALO + j + n], scalar=cdw_s[:, wof + j:wof + j + 1], in1=dw[:, c, 0:n],
                                op0=ALU.mult, op1=ALU.add), reads=[glb], writes=[dwb[c]])
                    fw.op("act", lambda e: e.copy(out=ct[:, :, 0:n], in_=dw[:, :, 0:n]), reads=dwb, writes=[ctb])
                    fw.op("act", lambda e: e.activation(out=mt[:, :, 0:n], in_=dw[:, :, 0:n], func=AF.Square), reads=dwb, writes=[mtb])
                    p1, p1b = mmbank()
                    p2, p2b = mmbank()
                    for pt, pb, src, srcb in ((p1, p1b, ct, ctb), (p2, p2b, mt, mtb)):
                        def mm(e, pt=pt, src=src):
                            r = None
                            for dc in range(DC):
                                r = e.matmul(pt[:, 0:n], lhsT=ones_b[:], rhs=src[:, dc, 0:n], start=(dc == 0), stop=(dc == DC - 1))
                            return r
                        fw.op("pe", mm, reads=[srcb], writes=[pb])
                    fw.op("dve", lambda e, p1=p1: e.tensor_scalar(out=st1[:, 0:n], in0=p1[:, 0:n], scalar1=1.0 / D, scalar2=None, op0=ALU.mult),
                          reads=[p1b], writes=[st1b])
                    fw.op("dve", lambda e: e.tensor_tensor(out=st2[:, 0:n], in0=st1[:, 0:n], in1=st1[:, 0:n], op=ALU.mult), reads=[st1b], writes=[st2b])
                    fw.op("dve", lambda e, p2=p2: e.scalar_tensor_tensor(out=st2[:, 0:n], in0=p2[:, 0:n], scalar=1.0 / D, in1=st2[:, 0:n],
                                                                         op0=ALU.mult, op1=ALU.subtract), reads=[p2b, st2b], writes=[st2b])
                    fw.op("act", lambda e: e.activation(out=st2[:, 0:n], in_=st2[:, 0:n], func=AF.Sqrt, scale=1.0, bias=eps_s[:, 1:2]),
                          reads=[st2b], writes=[st2b])
                    fw.op("dve", lambda e: e.reciprocal(out=st2[:, 0:n], in_=st2[:, 0:n]), reads=[st2b], writes=[st2b])
                    for c in range(DC):
                        fw.op("dve", lambda e, c=c: e.tensor_tensor(out=dw[:, c, 0:n], in0=dw[:, c, 0:n], in1=st1[:, 0:n], op=ALU.subtract),
                              reads=[st1b], writes=[dwb[c]])
                        fw.op("dve", lambda e, c=c: e.tensor_tensor(out=dw[:, c, 0:n], in0=dw[:, c, 0:n], in1=st2[:, 0:n], op=ALU.mult),
                              reads=[st2b], writes=[dwb[c]])
                        fw.op("act", lambda e, c=c: e.activation(out=ct[:, c, 0:n], in_=dw[:, c, 0:n], func=AF.Silu,
                                                                 scale=G_("lng", l, c), bias=G_("lnb", l, c)), reads=[dwb[c]], writes=[ctb])
                    for j in range(DC):
                        pa, pab = proj(0, j, ct, ctb, n)
                        pg, pgb = proj(1024, j, ht, hb, n)
                        sgt, sgb_, _ = sg.next()
                        fw.op("act", lambda e, pg=pg, sgt=sgt: e.activation(out=sgt[:, 0:n], in_=pg[:, 0:n], func=AF.Sigmoid), reads=[pgb], writes=[sgb_])
                        gat, gab, _ = ga.next()
                        fw.op("dve", lambda e, pa=pa, sgt=sgt, gat=gat: e.tensor_tensor(out=gat[:, 0:n], in0=pa[:, 0:n], in1=sgt[:, 0:n], op=ALU.mult),
                              reads=[pab, sgb_], writes=[gab])
                        pb2, pb2b = proj(3072, j, oit, oib, n)
                        pg2, pg2b = proj(2048, j, ht, hb, n)
                        sgt2, sgb2, _ = sg.next()
                        fw.op("act", lambda e, pg2=pg2, sgt2=sgt2: e.activation(out=sgt2[:, 0:n], in_=pg2[:, 0:n], func=AF.Sigmoid), reads=[pg2b], writes=[sgb2])
                        gbt, gbb, _ = gbm.next()
                        fw.op("dve", lambda e, pb2=pb2, sgt2=sgt2, gbt=gbt: e.tensor_tensor(out=gbt[:, 0:n], in0=pb2[:, 0:n], in1=sgt2[:, 0:n], op=ALU.mult),
                              reads=[pb2b, sgb2], writes=[gbb])
                        fw.op("dve", lambda e, gat=gat, gbt=gbt, j=j: e.tensor_tensor(out=mt[:, j, 0:n], in0=gat[:, 0:n], in1=gbt[:, 0:n], op=ALU.add),
                              reads=[gab, gbb], writes=[mtb])
                    for j in range(DC):
                        po_, pob = proj(4096, j, mt, mtb, n)
                        fw.op("dve", lambda e, po_=po_, xt=xt, j=j: e.tensor_tensor(out=xt[:, j, 0:n], in0=xt[:, j, 0:n], in1=po_[:, 0:n], op=ALU.add),
                              reads=[pob], writes=[xb])
                    fw.op("sp", lambda e, xt=xt, t0=t0, n=n: e.dma_start(out=X1[:, :, t0:t0 + n], in_=xt[:, :, 0:n]), reads=[xb], dsem=xs)
                for (t0, n) in tiles:
                    tileD(t0, n)
                fw.end()
            if _stop == "D%d" % l:
                break

            with contextlib.ExitStack() as ph:
                fw.begin(ph)
                reset_psum_bufs()
                NE = 256
                W1 = fw.sb("we1", [128, DC * 2 * DFF], BF16)
                W2 = fw.sb("we2", [128, FC * D], BF16)
                Wb = Buf()
                ws = fw.pdsem(True)
                load_w(W1, wE1[l], DC * 2 * DFF, Wb, ws)
                load_w(W2, wE2[l], FC * D, Wb, ws)
                W1v = W1[:].rearrange("p (d c) -> p d c", d=DC)
                W2v = W2[:].rearrange("p (f c) -> p f c", f=FC)
                xr = Ring(fw, "ex", [128, DC, NE], F32, 2)
                ht = fw.sb("eh", [128, DC, NE], BF16); hb = Buf()
                sq = fw.sb("esq", [128, DC, NE], BF16); sqb = Buf()
                rs = fw.sb("ers", [128, NE], F32); rsb = Buf()
                at = fw.sb("eat", [128, FC, NE], BF16); atb = Buf()
                sr = Ring(fw, "es", [128, NE], F32, 2, dma=False)
                last = (l == L - 1)
                if last:
                    yt = fw.sb("eyt", [128, DC, NE], F32); ytb = Buf()
                    yo = Ring(fw, "eyo", [128, D], F32, 2)
                etiles = [(j * NE, NE) for j in range(SEQ // NE)] + [(SEQ + 32 * s_, 32) for s_ in range(NS)]
                def tileE(t0, n):
                    xt, xb, xs = xr.next()
                    fw.op("sp", lambda e, xt=xt, t0=t0, n=n: e.dma_start(out=xt[:, :, 0:n], in_=X1[:, :, t0:t0 + n]), writes=[xb], dsem=xs)
                    rmsnorm(xt, xb, n, lambda dc: G_("ffn", l, dc), ht, hb, sq, sqb, rs, rsb)
                    for f in range(FC):
                        pg, pgb = mmbank()
                        pu, pub = mmbank()
                        for pt, pb, c0 in ((pg, pgb, f * 128), (pu, pub, DFF + f * 128)):
                            def mm(e, pt=pt, c0=c0):
                                r = None
                                for dc in range(DC):
                                    r = e.matmul(pt[:, 0:n], lhsT=W1v[:, dc, c0:c0 + 128], rhs=ht[:, dc, 0:n], start=(dc == 0), stop=(dc == DC - 1))
                                return r
                            fw.op("pe", mm, reads=[hb, Wb], writes=[pb])
                        st_, stb, _ = sr.next()
                        fw.op("act", lambda e, pg=pg, st_=st_: e.activation(out=st_[:, 0:n], in_=pg[:, 0:n], func=AF.Silu), reads=[pgb], writes=[stb])
                        fw.op("dve", lambda e, pu=pu, st_=st_, f=f: e.tensor_tensor(out=at[:, f, 0:n], in0=pu[:, 0:n], in1=st_[:, 0:n], op=ALU.mult),
                              reads=[pub, stb], writes=[atb])
                    for j in range(DC):
                        pt, pb = mmbank()

                        def mm(e, pt=pt, j=j):
                            r = None
                            for f in range(FC):
                                r = e.matmul(pt[:, 0:n], lhsT=W2v[:, f, j * 128:(j + 1) * 128], rhs=at[:, f, 0:n], start=(f == 0), stop=(f == FC - 1))
                            return r
                        fw.op("pe", mm, reads=[atb, Wb], writes=[pb])
                        fw.op("dve", lambda e, pt=pt, xt=xt, j=j: e.tensor_tensor(out=xt[:, j, 0:n], in0=xt[:, j, 0:n], in1=pt[:, 0:n], op=ALU.add),
                              reads=[pb], writes=[xb])
                    if not last:
                        fw.op("sp", lambda e, xt=xt, t0=t0, n=n: e.dma_start(out=X0[:, :, t0:t0 + n], in_=xt[:, :, 0:n]), reads=[xb], dsem=xs)
                    else:
                        fw.op("act", lambda e, xt=xt: e.activation(out=sq[:, :, 0:n], in_=xt[:, :, 0:n], func=AF.Square), reads=[xb], writes=[sqb])
                        pt, pb = mmbank()

                        def mm(e, pt=pt):
                            r = None
                            for dc in range(DC):
                                r = e.matmul(pt[:, 0:n], lhsT=ones_b[:], rhs=sq[:, dc, 0:n], start=(dc == 0), stop=(dc == DC - 1))
                            return r
                        fw.op("pe", mm, reads=[sqb], writes=[pb])
                        fw.op("act", lambda e, pt=pt: e.activation(out=rs[:, 0:n], in_=pt[:, 0:n], func=AF.Sqrt, scale=1.0 / D, bias=eps_s[:, 0:1]),
                              reads=[pb], writes=[rsb])
                        fw.op("dve", lambda e: e.reciprocal(out=rs[:, 0:n], in_=rs[:, 0:n]), reads=[rsb], writes=[rsb])
                        gfin = 5 * L * DC
                        for dc in range(DC):
                            fw.op("dve", lambda e, dc=dc, xt=xt: e.scalar_tensor_tensor(
                                out=yt[:, dc, 0:n], in0=xt[:, dc, 0:n], scalar=gv[:, gfin + dc:gfin + dc + 1], in1=rs[:, 0:n],
                                op0=ALU.mult, op1=ALU.mult), reads=[xb, rsb], writes=[ytb])
                        mtok = min(n, 128)
                        for sbi in range(max(1, n // 128)):
                            yot, yob, yos = yo.next()
                            for half in range(2):
                                pt, pb = mmbank()

                                def tp(e, pt=pt, half=half, sbi=sbi):
                                    r = None
                                    for c4 in range(4):
                                        dc = half * 4 + c4
                                        r = e.transpose(out=pt[0:mtok, c4 * 128:(c4 + 1) * 128], in_=yt[:, dc, sbi * 128:sbi * 128 + mtok],
                                                        identity=ident_f[:])
                                    return r
                                fw.op("pe", tp, reads=[ytb], writes=[pb])
                                fw.op("act", lambda e, pt=pt, yot=yot, half=half: e.copy(out=yot[0:mtok, half * 512:(half + 1) * 512], in_=pt[0:mtok, :]),
                                      reads=[pb], writes=[yob])
                            dst = y_s[(t0 - SEQ) // 32, 0:mtok, :] if t0 >= SEQ else y_p[t0 + sbi * 128:t0 + sbi * 128 + 128, :]
                            fw.op("sp", lambda e, dst=dst, yot=yot: e.dma_start(out=dst, in_=yot[0:mtok, :]), reads=[yob], dsem=yos)
                for (t0, n) in etiles:
                    tileE(t0, n)
                fw.end()
            if _stop == "E%d" % l:
                break
    return nc


def _prep_inputs(inp, SEQ, PAST, L):
    f = np.float32
    A = lambda a: np.ascontiguousarray(np.asarray(a, dtype=f))

    def fm(x):
        return A(x.reshape(x.shape[0], DC, 128).transpose(2, 1, 0))

    def wl(w):
        k, n = w.shape
        return A(w.reshape(k // 128, 128, n).transpose(1, 0, 2).reshape(128, -1))

    def vecs(v):
        return np.asarray(v, f).reshape(-1, DC, 128).transpose(2, 0, 1).reshape(128, -1)

    OQ, OK_, OV, OG = 2048, 3072, 4096, 5120
    w_in = np.asarray(inp["w_in"], f)
    wA = np.stack([wl(np.concatenate([w_in[l][:, OK_:OV], w_in[l][:, OV:OG], w_in[l][:, 0:2048], w_in[l][:, OQ:OK_]], axis=1)) for l in range(L)])
    wD = np.stack([wl(np.concatenate([np.asarray(inp["w_conv_out"][l], f), w_in[l][:, OG:OG + 2048],
                                      np.asarray(inp["w_attn_o"][l], f), np.asarray(inp["w_out"][l], f)], axis=1)) for l in range(L)])
    wE1 = np.stack([wl(np.asarray(inp["w_ffn_in"][l], f)) for l in range(L)])
    wE2 = np.stack([wl(np.asarray(inp["w_ffn_out"][l], f)) for l in range(L)])
    gvec = np.concatenate([vecs(inp["norm_mix"]), vecs(inp["norm_ffn"]), vecs(inp["conv_ln_g"]), vecs(inp["conv_ln_b"]),
                           vecs(inp["conv_dw_b"]), vecs(np.asarray(inp["norm_final"])[None])], axis=1)
    cdw = np.asarray(inp["conv_dw"], f).reshape(L, CW, DC, 128).transpose(3, 0, 2, 1).reshape(128, -1)
    subg = np.broadcast_to(np.asarray(inp["attn_subln_g"], f).reshape(1, -1), (128, L * 128))
    lamv = np.broadcast_to(np.stack([np.asarray(inp[k], f) for k in ("lambda_q1", "lambda_k1", "lambda_q2", "lambda_k2")]).reshape(1, -1), (128, 4 * L * 64))
    rbv = np.broadcast_to(np.asarray(inp["rel_bias"], f).reshape(1, -1), (128, 32 * H))
    shared = dict(xT_p=fm(np.asarray(inp["x_prompt"], f)[0]), wA=A(wA), wD=A(wD), wE1=A(wE1), wE2=A(wE2), gvec=A(gvec), cdw=A(cdw),
                  subg=A(subg), lamv=A(lamv), rbv=A(rbv), identf=np.eye(128, dtype=f))
    for k, (oh, ng) in _bias_tables(PAST).items():
        shared["oh_" + k] = A(oh)
        shared["ng_" + k] = A(ng)
    ck = np.asarray(inp["cache_k"], f)
    cv = np.asarray(inp["cache_v"], f)
    sc = np.asarray(inp["state_conv"], f)
    xs = np.asarray(inp["x_sample"], f)
    m = dict(shared)
    m["xT_s"] = fm(xs.reshape(NCORES * 32, D))
    m["kcT"] = A(ck.transpose(0, 1, 3, 4, 2))
    m["vc"] = A(cv.reshape(L, NCORES, PAST // 128, 128, H, 128).transpose(0, 1, 4, 3, 2, 5))
    m["scT"] = A(sc.reshape(L, NCORES, HALO, DC, 128).transpose(0, 1, 4, 3, 2))
    return [m]


def _run(inp, SEQ, PAST, L):
    nc = build(SEQ, PAST, L)
    maps = _prep_inputs(inp, SEQ, PAST, L)
    res = run_bass_kernel_spmd(nc, maps, core_ids=[0])
    r = res.results[0]
    g = lambda k: np.asarray(r[k], np.float32)
    y_prompt = g("y_p")[None]
    k_prompt = g("k_p").reshape(L, 1, SEQ, H, 128)
    v_prompt = g("v_p").reshape(L, 1, SEQ, H, 128)
    conv_prompt = g("c_p")[:, None]
    y_sample = g("y_s")
    k_sample = g("k_s").reshape(L, NCORES, 32, H, 128)
    v_sample = g("v_s").reshape(L, NCORES, 32, H, 128)
    conv_sample = g("c_s")
    return (y_prompt, y_sample, k_prompt, v_prompt, conv_prompt, k_sample, v_sample, conv_sample)


def kernel(**inputs):
    return _run(inputs, 16384, 4096, 4)
```
